# Optimizing a Trainium2 kernel written in Bass

```python
import math
import jax
import jax.numpy as jnp
from jax import lax
import numpy as np

D_MODEL = 2048
BATCH = 4
SEQ = 4096
DEPTH = 2

GRID_W = 64
CTX_LEN = 256
HEAD_DIM = 128
D_MIX = D_MODEL
GROUP_W = D_MIX // 4
N_GROUP_HEADS = GROUP_W // HEAD_DIM
D_FF = 4 * D_MODEL
Q_BLOCK = 128
ROPE_THETA = 10000.0
NORM_EPS = 1e-6

MLA_HEADS = N_GROUP_HEADS
MLA_Q_RANK = GROUP_W
MLA_KV_RANK = GROUP_W // 2
MLA_NOPE = 128
MLA_ROPE = 64
MLA_V = GROUP_W // MLA_HEADS
GQA_Q_HEADS = N_GROUP_HEADS
GQA_KV_HEADS = N_GROUP_HEADS // 2
GQA_GROUP = GQA_Q_HEADS // GQA_KV_HEADS
DIFF_HEADS = N_GROUP_HEADS
DIFF_HALF = HEAD_DIM // 2
NA_HEADS = N_GROUP_HEADS
NA_WIN_H = 8
NA_WIN_W = 16

A_COLS = MLA_Q_RANK + MLA_KV_RANK + MLA_ROPE
B_COLS = (GQA_Q_HEADS + 2 * GQA_KV_HEADS) * HEAD_DIM
C_COLS = 3 * DIFF_HEADS * HEAD_DIM
D_COLS = 3 * NA_HEADS * HEAD_DIM
IN_COLS = A_COLS + B_COLS + C_COLS + D_COLS

kernel_name = 'hybrid_parallel_mixer_dit_block'


def rmsnorm(x, g):
    xf = x.astype(jnp.float32)
    y = xf * lax.rsqrt(jnp.mean(xf * xf, axis=-1, keepdims=True) + NORM_EPS)
    return (y * g.astype(jnp.float32)).astype(x.dtype)


def axial_angles(n_tok, rot_dim):
    t = jnp.arange(n_tok)
    row = (t // GRID_W).astype(jnp.float32)
    col = (t % GRID_W).astype(jnp.float32)
    half = rot_dim // 2
    inv_freq = ROPE_THETA ** (-jnp.arange(0, half, 2, dtype=jnp.float32) / half)
    return row[:, None] * inv_freq, col[:, None] * inv_freq


def _rope_1d(x, ang):
    x1, x2 = jnp.split(x, 2, axis=-1)
    cos = jnp.cos(ang).astype(x.dtype)
    sin = jnp.sin(ang).astype(x.dtype)
    return jnp.concatenate([x1 * cos - x2 * sin, x2 * cos + x1 * sin], axis=-1)


def axial_rope(x, angs):
    ang_row, ang_col = angs
    half = x.shape[-1] // 2
    return jnp.concatenate([_rope_1d(x[..., :half], ang_row), _rope_1d(x[..., half:], ang_col)], axis=-1)


def to_blocks(a):
    nb = a.shape[-2] // Q_BLOCK
    a = a.reshape(a.shape[:-2] + (nb, Q_BLOCK, a.shape[-1]))
    return jnp.moveaxis(a, -3, 0)


def from_blocks(o):
    o = jnp.moveaxis(o, 0, -3)
    return o.reshape(o.shape[:-3] + (o.shape[-3] * o.shape[-2], o.shape[-1]))


def merge_heads(o):
    b, h, n, d = o.shape
    return o.transpose(0, 2, 1, 3).reshape(b, n, h * d)


def _softmax(s):
    return jax.nn.softmax(s.astype(jnp.float32), axis=-1)


def sq_relu_mlp(h, w_up, w_down):
    return jnp.square(jax.nn.relu(h @ w_up)) @ w_down


def mla_mixer(p_lat, p_ctx, g_qa, g_kva, w_uq, w_ukv, g_q, g_k, angs, need_ctx):
    def project(p):
        cq = rmsnorm(p[..., :MLA_Q_RANK], g_qa)
        ckv = rmsnorm(p[..., MLA_Q_RANK:MLA_Q_RANK + MLA_KV_RANK], g_kva)
        k_pe = rmsnorm(p[..., MLA_Q_RANK + MLA_KV_RANK:], g_k[MLA_NOPE:])
        q = jnp.einsum('bnr,rhd->bhnd', cq, w_uq.reshape(MLA_Q_RANK, MLA_HEADS, MLA_NOPE + MLA_ROPE))
        kv = jnp.einsum('bnr,rhd->bhnd', ckv, w_ukv.reshape(MLA_KV_RANK, MLA_HEADS, MLA_NOPE + MLA_V))
        q_nope = rmsnorm(q[..., :MLA_NOPE], g_q[:MLA_NOPE])
        q_pe = rmsnorm(q[..., MLA_NOPE:], g_q[MLA_NOPE:])
        k_nope = rmsnorm(kv[..., :MLA_NOPE], g_k[:MLA_NOPE])
        return q_nope, q_pe, k_nope, k_pe, kv[..., MLA_NOPE:]

    scale = (MLA_NOPE + MLA_ROPE) ** -0.5

    def attend(qn, qr, kn, kr, v):
        s = (jnp.einsum('bhqd,bhkd->bhqk', qn, kn) + jnp.einsum('bhqr,bkr->bhqk', qr, kr)).astype(jnp.float32) * scale
        return jnp.einsum('bhqk,bhkd->bhqd', _softmax(s).astype(v.dtype), v)

    qn, qr, kn, kr, v = project(p_lat)
    qr = axial_rope(qr, angs)
    kr = axial_rope(kr, angs)
    cqn, cqr, ckn, ckr, cv = project(p_ctx)
    kn_all = jnp.concatenate([kn, ckn], axis=2)
    kr_all = jnp.concatenate([kr, ckr], axis=1)
    v_all = jnp.concatenate([v, cv], axis=2)
    o = lax.map(lambda qb: attend(qb[0], qb[1], kn_all, kr_all, v_all), (to_blocks(qn), to_blocks(qr)))
    o_lat = merge_heads(from_blocks(o))
    o_ctx = merge_heads(attend(cqn, cqr, ckn, ckr, cv)) if need_ctx else None
    return o_lat, o_ctx


def gqa_mixer(p_lat, p_ctx, g_q, g_k, angs, need_ctx):
    qd = GQA_Q_HEADS * HEAD_DIM
    kd = GQA_KV_HEADS * HEAD_DIM

    def project(p):
        b, n, _ = p.shape
        q = p[..., :qd].reshape(b, n, GQA_KV_HEADS, GQA_GROUP, HEAD_DIM).transpose(0, 2, 3, 1, 4)
        k = p[..., qd:qd + kd].reshape(b, n, GQA_KV_HEADS, HEAD_DIM).transpose(0, 2, 1, 3)
        v = p[..., qd + kd:].reshape(b, n, GQA_KV_HEADS, HEAD_DIM).transpose(0, 2, 1, 3)
        return rmsnorm(q, g_q), rmsnorm(k, g_k), v

    def attend(q, k, v):
        s = jnp.einsum('bkgqd,bknd->bkgqn', q, k).astype(jnp.float32) * HEAD_DIM ** -0.5
        return jnp.einsum('bkgqn,bknd->bkgqd', _softmax(s).astype(v.dtype), v)

    def heads_out(o):
        b, kh, g, n, d = o.shape
        return merge_heads(o.reshape(b, kh * g, n, d))

    q, k, v = project(p_lat)
    q = axial_rope(q, angs)
    k = axial_rope(k, angs)
    cq, ck, cv = project(p_ctx)
    k_all = jnp.concatenate([k, ck], axis=2)
    v_all = jnp.concatenate([v, cv], axis=2)
    o = lax.map(lambda qb: attend(qb, k_all, v_all), to_blocks(q))
    o_lat = heads_out(from_blocks(o))
    o_ctx = heads_out(attend(cq, ck, cv)) if need_ctx else None
    return o_lat, o_ctx


def diff_mixer(p_lat, p_ctx, g_q, g_k, lq1, lk1, lq2, lk2, g_out, angs, lam_init, need_ctx):
    hd = DIFF_HEADS * HEAD_DIM

    def project(p):
        b, n, _ = p.shape
        q = p[..., :hd].reshape(b, n, DIFF_HEADS, 2, DIFF_HALF).transpose(0, 2, 3, 1, 4)
        k = p[..., hd:2 * hd].reshape(b, n, DIFF_HEADS, 2, DIFF_HALF).transpose(0, 2, 3, 1, 4)
        v = p[..., 2 * hd:].reshape(b, n, DIFF_HEADS, HEAD_DIM).transpose(0, 2, 1, 3)
        return rmsnorm(q, g_q), rmsnorm(k, g_k), v

    f32 = jnp.float32
    lam = (jnp.exp(jnp.sum(lq1.astype(f32) * lk1.astype(f32)))
           - jnp.exp(jnp.sum(lq2.astype(f32) * lk2.astype(f32))) + lam_init)

    def attend(q, k, v):
        s = jnp.einsum('bhiqd,bhind->bhiqn', q, k).astype(f32) * DIFF_HALF ** -0.5
        a = _softmax(s)
        w = a[:, :, 0] - lam * a[:, :, 1]
        o = jnp.einsum('bhqn,bhnd->bhqd', w.astype(v.dtype), v)
        return rmsnorm(o, g_out) * (1.0 - lam_init)

    q, k, v = project(p_lat)
    q = axial_rope(q, angs)
    k = axial_rope(k, angs)
    cq, ck, cv = project(p_ctx)
    k_all = jnp.concatenate([k, ck], axis=3)
    v_all = jnp.concatenate([v, cv], axis=2)
    o = lax.map(lambda qb: attend(qb, k_all, v_all), to_blocks(q))
    o_lat = merge_heads(from_blocks(o))
    o_ctx = merge_heads(attend(cq, ck, cv)) if need_ctx else None
    return o_lat, o_ctx


def na_mixer(p_lat, p_ctx, g_q, g_k, rpb, need_ctx):
    hd = NA_HEADS * HEAD_DIM
    f32 = jnp.float32
    scale = HEAD_DIM ** -0.5

    def project(p):
        b, n, _ = p.shape
        q = p[..., :hd].reshape(b, n, NA_HEADS, HEAD_DIM).transpose(0, 2, 1, 3)
        k = p[..., hd:2 * hd].reshape(b, n, NA_HEADS, HEAD_DIM).transpose(0, 2, 1, 3)
        v = p[..., 2 * hd:].reshape(b, n, NA_HEADS, HEAD_DIM).transpose(0, 2, 1, 3)
        return rmsnorm(q, g_q), rmsnorm(k, g_k), v

    q, k, v = project(p_lat)
    cq, ck, cv = project(p_ctx)
    n_tok = q.shape[2]
    rows = n_tok // GRID_W
    wh = min(NA_WIN_H, rows)
    bh = min(wh + 1, rows)
    band = bh * GRID_W
    rows_per_block = Q_BLOCK // GRID_W
    q_loc = jnp.arange(Q_BLOCK)
    q_roff = q_loc // GRID_W
    q_col = q_loc % GRID_W
    k_roff = jnp.repeat(jnp.arange(bh), GRID_W)
    k_col = jnp.tile(jnp.arange(GRID_W), bh)
    col_start = jnp.clip(q_col - NA_WIN_W // 2, 0, GRID_W - NA_WIN_W)
    col_mask = (k_col[None, :] >= col_start[:, None]) & (k_col[None, :] < col_start[:, None] + NA_WIN_W)
    col_idx = jnp.clip(k_col[None, :] - q_col[:, None] + NA_WIN_W - 1, 0, 2 * NA_WIN_W - 2)

    def block(args):
        i, qb = args
        r0 = i * rows_per_block
        q_row = r0 + q_roff
        row_start = jnp.clip(q_row - wh // 2, 0, rows - wh)
        band_start = jnp.minimum(jnp.clip(r0 - wh // 2, 0, rows - wh), rows - bh)
        kb = lax.dynamic_slice_in_dim(k, band_start * GRID_W, band, axis=2)
        vb = lax.dynamic_slice_in_dim(v, band_start * GRID_W, band, axis=2)
        k_row = band_start + k_roff
        mask = col_mask & (k_row[None, :] >= row_start[:, None]) & (k_row[None, :] < row_start[:, None] + wh)
        row_idx = jnp.clip(k_row[None, :] - q_row[:, None] + NA_WIN_H - 1, 0, 2 * NA_WIN_H - 2)
        bias = rpb[:, row_idx, col_idx].astype(f32)
        s_band = jnp.einsum('bhqd,bhkd->bhqk', qb, kb).astype(f32) * scale + bias
        s_band = jnp.where(mask, s_band, -jnp.inf)
        s_ctx = jnp.einsum('bhqd,bhkd->bhqk', qb, ck).astype(f32) * scale
        pr = _softmax(jnp.concatenate([s_band, s_ctx], axis=-1)).astype(v.dtype)
        return (jnp.einsum('bhqk,bhkd->bhqd', pr[..., :band], vb)
                + jnp.einsum('bhqk,bhkd->bhqd', pr[..., band:], cv))

    o = lax.map(block, (jnp.arange(n_tok // Q_BLOCK), to_blocks(q)))
    o_lat = merge_heads(from_blocks(o))
    if need_ctx:
        s = jnp.einsum('bhqd,bhkd->bhqk', cq, ck).astype(f32) * scale
        o_ctx = merge_heads(jnp.einsum('bhqk,bhkd->bhqd', _softmax(s).astype(cv.dtype), cv))
    else:
        o_ctx = None
    return o_lat, o_ctx


def setup_inputs(seed: int = 0) -> dict:
    key = jax.random.key(seed)
    ks = jax.random.split(key, 32)
    f32 = jnp.float32

    def nrm(k, shape, s):
        return jax.random.normal(k, shape, f32) * s

    def gain(k, shape):
        return 1.0 + 0.05 * jax.random.normal(k, shape, f32)

    return {
        'x': nrm(ks[0], (BATCH, SEQ, D_MODEL), 1.0),
        'c': nrm(ks[1], (BATCH, D_MODEL), 1.0),
        'ctx': nrm(ks[2], (BATCH, CTX_LEN, D_MODEL), 1.0),
        'c_ctx': nrm(ks[3], (D_MODEL,), 1.0),
        'w_mod': nrm(ks[4], (DEPTH, D_MODEL, 6 * D_MODEL), D_MODEL ** -0.5),
        'b_mod': nrm(ks[5], (DEPTH, 6 * D_MODEL), 0.02),
        'g_norm_mix': gain(ks[6], (DEPTH, D_MODEL)),
        'g_norm_mlp': gain(ks[7], (DEPTH, D_MODEL)),
        'w_in': nrm(ks[8], (DEPTH, D_MODEL, IN_COLS), D_MODEL ** -0.5),
        'mla_g_qa': gain(ks[9], (DEPTH, MLA_Q_RANK)),
        'mla_g_kva': gain(ks[10], (DEPTH, MLA_KV_RANK)),
        'mla_w_uq': nrm(ks[11], (DEPTH, MLA_Q_RANK, MLA_HEADS * (MLA_NOPE + MLA_ROPE)), MLA_Q_RANK ** -0.5),
        'mla_w_ukv': nrm(ks[12], (DEPTH, MLA_KV_RANK, MLA_HEADS * (MLA_NOPE + MLA_V)), MLA_KV_RANK ** -0.5),
        'mla_g_q': gain(ks[13], (DEPTH, MLA_NOPE + MLA_ROPE)),
        'mla_g_k': gain(ks[14], (DEPTH, MLA_NOPE + MLA_ROPE)),
        'gqa_g_q': gain(ks[15], (DEPTH, HEAD_DIM)),
        'gqa_g_k': gain(ks[16], (DEPTH, HEAD_DIM)),
        'diff_g_q': gain(ks[17], (DEPTH, DIFF_HALF)),
        'diff_g_k': gain(ks[18], (DEPTH, DIFF_HALF)),
        'diff_lq1': nrm(ks[19], (DEPTH, DIFF_HALF), 0.1),
        'diff_lk1': nrm(ks[20], (DEPTH, DIFF_HALF), 0.1),
        'diff_lq2': nrm(ks[21], (DEPTH, DIFF_HALF), 0.1),
        'diff_lk2': nrm(ks[22], (DEPTH, DIFF_HALF), 0.1),
        'diff_g_out': gain(ks[23], (DEPTH, HEAD_DIM)),
        'na_g_q': gain(ks[24], (DEPTH, HEAD_DIM)),
        'na_g_k': gain(ks[25], (DEPTH, HEAD_DIM)),
        'na_rpb': nrm(ks[26], (DEPTH, NA_HEADS, 2 * NA_WIN_H - 1, 2 * NA_WIN_W - 1), 0.1),
        'w_out': nrm(ks[27], (DEPTH, D_MIX, D_MODEL), D_MIX ** -0.5),
        'w_up': nrm(ks[28], (DEPTH, D_MODEL, D_FF), D_MODEL ** -0.5),
        'w_down': nrm(ks[29], (DEPTH, D_FF, D_MODEL), D_FF ** -0.5),
    }


def reference(x, c, ctx, c_ctx, w_mod, b_mod, g_norm_mix, g_norm_mlp, w_in, mla_g_qa, mla_g_kva,
              mla_w_uq, mla_w_ukv, mla_g_q, mla_g_k, gqa_g_q, gqa_g_k, diff_g_q, diff_g_k,
              diff_lq1, diff_lk1, diff_lq2, diff_lk2, diff_g_out, na_g_q, na_g_k, na_rpb,
              w_out, w_up, w_down):
    n_tok = x.shape[1]
    angs_mla = axial_angles(n_tok, MLA_ROPE)
    angs_gqa = axial_angles(n_tok, HEAD_DIM)
    angs_diff = axial_angles(n_tok, DIFF_HALF)
    ca, cb, cc = A_COLS, A_COLS + B_COLS, A_COLS + B_COLS + C_COLS
    for l in range(DEPTH):
        need_ctx = l < DEPTH - 1
        mod = jax.nn.silu(c) @ w_mod[l] + b_mod[l]
        mod_c = jax.nn.silu(c_ctx) @ w_mod[l] + b_mod[l]
        sh_a, sc_a, gt_a, sh_m, sc_m, gt_m = jnp.split(mod[:, None, :], 6, axis=-1)
        csh_a, csc_a, cgt_a, csh_m, csc_m, cgt_m = jnp.split(mod_c, 6, axis=-1)
        p = (rmsnorm(x, g_norm_mix[l]) * (1 + sc_a) + sh_a) @ w_in[l]
        pc = (rmsnorm(ctx, g_norm_mix[l]) * (1 + csc_a) + csh_a) @ w_in[l]
        o_a, oc_a = mla_mixer(p[..., :ca], pc[..., :ca], mla_g_qa[l], mla_g_kva[l], mla_w_uq[l],
                              mla_w_ukv[l], mla_g_q[l], mla_g_k[l], angs_mla, need_ctx)
        o_b, oc_b = gqa_mixer(p[..., ca:cb], pc[..., ca:cb], gqa_g_q[l], gqa_g_k[l], angs_gqa, need_ctx)
        o_c, oc_c = diff_mixer(p[..., cb:cc], pc[..., cb:cc], diff_g_q[l], diff_g_k[l], diff_lq1[l],
                               diff_lk1[l], diff_lq2[l], diff_lk2[l], diff_g_out[l], angs_diff,
                               0.8 - 0.6 * math.exp(-0.3 * l), need_ctx)
        o_d, oc_d = na_mixer(p[..., cc:], pc[..., cc:], na_g_q[l], na_g_k[l], na_rpb[l], need_ctx)
        x = x + gt_a * (jnp.concatenate([o_a, o_b, o_c, o_d], axis=-1) @ w_out[l])
        x = x + gt_m * sq_relu_mlp(rmsnorm(x, g_norm_mlp[l]) * (1 + sc_m) + sh_m, w_up[l], w_down[l])
        if need_ctx:
            ctx = ctx + cgt_a * (jnp.concatenate([oc_a, oc_b, oc_c, oc_d], axis=-1) @ w_out[l])
            ctx = ctx + cgt_m * sq_relu_mlp(rmsnorm(ctx, g_norm_mlp[l]) * (1 + csc_m) + csh_m,
                                            w_up[l], w_down[l])
    return x
```

```python
import math
from contextlib import ExitStack

import ml_dtypes
import numpy as np

import concourse.bass as bass
import concourse.mybir as mybir
from concourse.bass_utils import run_bass_kernel_spmd

F32, BF16 = mybir.dt.float32, mybir.dt.bfloat16
AF = mybir.ActivationFunctionType
ALU = mybir.AluOpType
AX = mybir.AxisListType

D = 2048
SEQ = 4096
BATCH = 4
CTX = 256
HALF = 2048
NB = 512
GRID_W = 64
EPS = 1e-6
D_FF = 8192
IN_COLS = 4928
NQ_TOK = HALF + CTX
NK_TOK = SEQ + CTX
NKT = NK_TOK // 128
NVH = 14
NEG = -30000.0

Q_MLA_N, Q_MLA_R, Q_GQA, Q_DIFF, Q_NA = 0, 512, 768, 1280, 1792
NQ_ROWS = 2304
K_MLA_N, K_MLA_R, K_GQA, K_DIFF, K_NA = 0, 512, 576, 832, 1344
NK_ROWS = 1856
GC_NMIX, GC_NMLP, GC_QA, GC_KVA, GC_MQN, GC_MQR, GC_MKN, GC_MKR = 0, 16, 32, 36, 38, 39, 40, 41
GC_GQ, GC_GK, GC_DQ, GC_DK, GC_NQ, GC_NK = 42, 43, 44, 45, 46, 47
NGC = 48


class Buf:
    __slots__ = ("name", "w", "r")

    def __init__(self, name=""):
        self.name = name
        self.w = None
        self.r = {}


class _Rec:
    def __init__(self):
        self.calls = []

    def __getattr__(self, name):
        def f(*a, **k):
            self.calls.append((name, a, k))
            return self
        return f


def _record(fn):
    r = _Rec()
    fn(r)
    assert len(r.calls) == 1, r.calls
    return r.calls[0]


class Prog:
    COMPUTE = ("pe", "act", "dve", "pool")

    def __init__(self, nc, stack, dma_slots=14):
        self.nc = nc
        self.stack = stack
        self.eng = {"pe": nc.tensor, "act": nc.scalar, "dve": nc.vector, "pool": nc.gpsimd, "sp": nc.sync}
        self.streams = {k: [] for k in self.eng}
        self.sems = {}
        self.cnt = {}
        self.cur = {}
        self.owner = {}
        self.waited = {k: {} for k in self.eng}
        self.n_instr = 0
        self.n_wait = 0
        for k in self.COMPUTE:
            key = f"c_{k}"
            self.sems[key] = stack.enter_context(nc.semaphore(key))
            self.cnt[key] = 0
            self.cur[k] = key
            self.owner[key] = k
        self.dma_slots = {}
        self.dma_next = {}
        for q in ("sp", "pool"):
            keys = []
            for i in range(dma_slots):
                key = f"d_{q}_{i}"
                self.sems[key] = stack.enter_context(nc.semaphore(key))
                self.cnt[key] = 0
                keys.append(key)
            self.dma_slots[q] = keys
            self.dma_next[q] = 0

    def _wait_force(self, engine, ticket):
        key, val = ticket
        if self.waited[engine].get(key, 0) >= val:
            return
        self.waited[engine][key] = val
        sem = self.sems[key]
        self.streams[engine].append(("wait_ge", (sem, val), {}, None))
        self.n_wait += 1

    def _wait(self, engine, ticket):
        if ticket is None:
            return
        if engine == "pe" and self.owner.get(ticket[0]) == "pe":
            return
        self._wait_force(engine, ticket)

    def _deps(self, engine, reads, writes):
        for b in reads:
            self._wait(engine, b.w)
        for b in writes:
            self._wait(engine, b.w)
            for key, val in b.r.items():
                self._wait(engine, (key, val))

    def _commit(self, ticket, reads, writes):
        key, val = ticket
        for b in writes:
            b.w = ticket
            b.r = {}
        for b in reads:
            if b.r.get(key, 0) < val:
                b.r[key] = val

    def emit(self, engine, fns, reads=(), writes=()):
        if callable(fns):
            fns = [fns]
        self._deps(engine, reads, writes)
        key = self.cur[engine]
        self.cnt[key] += 1
        val = self.cnt[key]
        sem = self.sems[key]
        st = self.streams[engine]
        for f in fns[:-1]:
            st.append(_record(f) + (None,))
        st.append(_record(fns[-1]) + ((sem, 1),))
        self.n_instr += len(fns)
        ticket = (key, val)
        self._commit(ticket, reads, writes)
        return ticket

    def dma(self, queue, fn, reads=(), writes=()):
        slots = self.dma_slots[queue]
        i = self.dma_next[queue]
        self.dma_next[queue] = (i + 1) % len(slots)
        key = slots[i]
        if self.cnt[key] > 0:
            self._wait_force(queue, (key, self.cnt[key]))
        self._deps(queue, reads, writes)
        self.cnt[key] += 16
        val = self.cnt[key]
        sem = self.sems[key]
        self.streams[queue].append(_record(fn) + ((sem, 16),))
        self.n_instr += 1
        ticket = (key, val)
        self._commit(ticket, reads, writes)
        return ticket

    def barrier(self, engines=None):
        for e in (engines or list(self.eng)):
            for key, val in self.cnt.items():
                if val > 0:
                    self._wait_force(e, (key, val))

    def flush(self):
        nc = self.nc
        streams = self.streams
        self.streams = {k: [] for k in self.eng}

        def play(e, lst):
            for (name, a, k, inc) in lst:
                ins = getattr(e, name)(*a, **k)
                if inc is not None:
                    ins.then_inc(inc[0], inc[1])

        with nc.Block() as block:
            @block.tensor
            def _(e):
                play(e, streams["pe"])

            @block.scalar
            def _(e):
                play(e, streams["act"])

            @block.vector
            def _(e):
                play(e, streams["dve"])

            @block.gpsimd
            def _(e):
                play(e, streams["pool"])

            @block.sync
            def _(e):
                play(e, streams["sp"])


class Ring:
    def __init__(self, nc, stack, name, n, shape, dtype):
        self.t = [stack.enter_context(nc.sbuf_tensor(f"{name}{i}", shape, dtype)) for i in range(n)]
        self.b = [Buf(f"{name}{i}") for i in range(n)]
        self.i = 0

    def next(self):
        i = self.i
        self.i = (i + 1) % len(self.t)
        return self.t[i], self.b[i]


class _Stop(Exception):
    pass


class LayerEmitter:
    stop = None

    def dbg(self, tag):
        if self.stop == tag:
            raise _Stop()

    def __init__(self, nc, P, stack, io, need_ctx, lam_init, lname):
        self.nc, self.P, self.io = nc, P, io
        self.need_ctx = need_ctx
        self.lam_init = lam_init
        self.ln = lname
        self.gstack = stack

    def sb(self, st, name, shape, dtype):
        return st.enter_context(self.nc.sbuf_tensor(f"{self.ln}_{name}", shape, dtype))

    def load_w(self, wring, src2d, ncols, krows=16):
        wt, wb = wring.next()
        src = src2d.rearrange("(k p) n -> p k n", p=128)
        for k0 in range(0, krows, 4):
            self.P.dma("pool", lambda e: e.dma_start(out=wt[:, k0:k0 + 4, 0:ncols], in_=src[:, k0:k0 + 4, :]), writes=[wb])
        return wt, wb

    def setup(self, st):
        nc, P, io = self.nc, self.P, self.io
        c = {}
        c["ones"] = self.sb(st, "ones", [128, 128], F32)
        c["bd64"] = self.sb(st, "bd64", [128, 128], F32)
        c["pg"] = self.sb(st, "pg", [128, 128], F32)
        c["pd"] = self.sb(st, "pd", [128, 128], F32)
        c["ident"] = self.sb(st, "ident", [128, 128], BF16)
        c["gc"] = self.sb(st, "gc", [128, NGC], F32)
        c["grow"] = self.sb(st, "grow", [128, 128], F32)
        c["lamv"] = self.sb(st, "lamv", [128, 4, 64], F32)
        c["eps"] = self.sb(st, "eps", [128, 1], F32)
        c["wuq"] = self.sb(st, "wuq", [128, 4, 768], BF16)
        c["wukv"] = self.sb(st, "wukv", [128, 2, 1024], BF16)
        c["modT"] = self.sb(st, "modT", [128, 96, 2], F32)
        c["bmod"] = self.sb(st, "bmod", [128, 96], F32)
        c["a1"] = self.sb(st, "a1", [128, 16, 2], F32)
        c["a2"] = self.sb(st, "a2", [128, 16, 2], F32)
        c["lam"] = self.sb(st, "lam", [128, 4], F32)
        self.nb_lt = self.sb(st, "nb_lt", [128, NB], F32)
        self.nb_rt = self.sb(st, "nb_rt", [128, NB], F32)
        self.nb_ltb, self.nb_rtb = Buf("nb_lt"), Buf("nb_rt")
        cb = Buf("consts")
        self.c, self.cb = c, cb
        cm = io["cmat"]
        for i, k in enumerate(["ones", "bd64", "pg", "pd"]):
            P.dma("sp", lambda e, k=k, i=i: e.dma_start(out=c[k][:], in_=cm[i]), writes=[cb])
        P.dma("sp", lambda e: e.dma_start(out=c["ident"][:], in_=io["ident"]), writes=[cb])
        P.dma("sp", lambda e: e.dma_start(out=c["gc"][:], in_=io["gcols"]), writes=[cb])
        P.dma("sp", lambda e: e.dma_start(out=c["grow"][:], in_=io["grow"]), writes=[cb])
        P.dma("sp", lambda e: e.dma_start(out=c["lamv"][:], in_=io["lamv"]), writes=[cb])
        P.dma("sp", lambda e: e.dma_start(out=c["bmod"][:], in_=io["bmodT"]), writes=[cb])
        P.dma("pool", lambda e: e.dma_start(out=c["wuq"][:], in_=io["w_uq"].rearrange("(k p) n -> p k n", p=128)), writes=[cb])
        P.dma("pool", lambda e: e.dma_start(out=c["wukv"][:], in_=io["w_ukv"].rearrange("(k p) n -> p k n", p=128)), writes=[cb])
        P.emit("dve", lambda e: e.memset(c["eps"][:], EPS), writes=[cb])

        with ExitStack() as s2:
            tmp = self.sb(s2, "lamtmp", [128, 2, 64], F32)
            tb = Buf()
            P.emit("dve", lambda e: e.tensor_tensor(out=tmp[:, 0, :], in0=c["lamv"][:, 0, :], in1=c["lamv"][:, 1, :], op=ALU.mult), reads=[cb], writes=[tb])
            P.emit("dve", lambda e: e.tensor_tensor(out=tmp[:, 1, :], in0=c["lamv"][:, 2, :], in1=c["lamv"][:, 3, :], op=ALU.mult), reads=[cb], writes=[tb])
            P.emit("dve", lambda e: e.tensor_reduce(out=c["lam"][:, 0:2], in_=tmp[:], axis=AX.X, op=ALU.add), reads=[tb], writes=[cb])
            P.emit("act", lambda e: e.activation(out=c["lam"][:, 0:2], in_=c["lam"][:, 0:2], func=AF.Exp), reads=[cb], writes=[cb])
            P.emit("dve", lambda e: e.tensor_tensor(out=c["lam"][:, 2:3], in0=c["lam"][:, 0:1], in1=c["lam"][:, 1:2], op=ALU.subtract), reads=[cb], writes=[cb])
            P.emit("dve", lambda e: e.tensor_scalar(out=c["lam"][:, 2:3], in0=c["lam"][:, 2:3], scalar1=float(self.lam_init), scalar2=None, op0=ALU.add), reads=[cb], writes=[cb])
            P.emit("dve", lambda e: e.tensor_scalar(out=c["lam"][:, 3:4], in0=c["lam"][:, 2:3], scalar1=-1.0, scalar2=None, op0=ALU.mult), reads=[cb], writes=[cb])

            sil = self.sb(s2, "sil", [128, 16, 2], F32)
            sig = self.sb(s2, "sig", [128, 16, 2], F32)
            sb_ = Buf()
            P.dma("sp", lambda e: e.dma_start(out=sil[:], in_=io["cc"]), writes=[sb_])
            P.emit("act", lambda e: e.activation(out=sig[:], in_=sil[:], func=AF.Sigmoid), reads=[sb_], writes=[tb])
            P.emit("dve", lambda e: e.tensor_tensor(out=sil[:], in0=sil[:], in1=sig[:], op=ALU.mult), reads=[tb, sb_], writes=[sb_])
            wm = [self.sb(s2, f"wm{i}", [128, 16, 512], F32) for i in range(2)]
            wmb = [Buf(), Buf()]
            ps = self.ps[0]
            psb = self.psb[0]
            for g in range(24):
                w, wb = wm[g % 2], wmb[g % 2]
                src = io["w_mod"][:, g * 512:(g + 1) * 512].rearrange("(k p) n -> p k n", p=128)
                P.dma("sp", lambda e, w=w, src=src: e.dma_start(out=w[:], in_=src), writes=[wb])
                fns = []
                for j in range(4):
                    n = g * 4 + j
                    for k in range(16):
                        fns.append(lambda e, w=w, j=j, k=k, n=n: e.matmul(ps[:, n * 2:n * 2 + 2], lhsT=w[:, k, j * 128:(j + 1) * 128], rhs=sil[:, k, :], start=(k == 0), stop=(k == 15)))
                P.emit("pe", fns, reads=[wb, sb_], writes=[psb])
            for j in range(2):
                P.emit("dve", lambda e, j=j: e.tensor_tensor(out=c["modT"][:, :, j], in0=ps[:, 0:192].rearrange("p (n j) -> p n j", j=2)[:, :, j], in1=c["bmod"][:], op=ALU.add), reads=[psb, cb], writes=[cb])
            for j in range(2):
                P.emit("dve", lambda e, j=j: e.scalar_tensor_tensor(out=c["a1"][:, :, j], in0=c["modT"][:, 16:32, j], scalar=1.0, in1=c["gc"][:, GC_NMIX:GC_NMIX + 16], op0=ALU.add, op1=ALU.mult), reads=[cb], writes=[cb])
                P.emit("dve", lambda e, j=j: e.scalar_tensor_tensor(out=c["a2"][:, :, j], in0=c["modT"][:, 64:80, j], scalar=1.0, in1=c["gc"][:, GC_NMLP:GC_NMLP + 16], op0=ALU.add, op1=ALU.mult), reads=[cb], writes=[cb])
            P.barrier()
            P.flush()

    def mod(self, which, chunk, j):
        return self.c["modT"][:, which * 16 + chunk, j:j + 1]

    def rstd_from(self, sq_list, M, ones_ap, inv_d, nt, wk):
        P, c = self.P, self.c
        sps, spb = self.stat_ps()
        fns = []
        n = len(sq_list)
        for i, (ap, b) in enumerate(sq_list):
            fns.append(lambda e, ap=ap, i=i: e.matmul(sps[0:M, 0:nt], lhsT=ones_ap, rhs=ap, start=(i == 0), stop=(i == n - 1)))
        P.emit("pe", fns, reads=[b for _, b in sq_list] + [self.cb], writes=[spb])
        lt, lb = wk.next()
        P.emit("act", lambda e: e.activation(out=lt[0:M, 0:nt], in_=sps[0:M, 0:nt], func=AF.Ln, bias=c["eps"][0:M, :], scale=float(inv_d)), reads=[spb, self.cb], writes=[lb])
        rt, rb = wk.next()
        P.emit("act", lambda e: e.activation(out=rt[0:M, 0:nt], in_=lt[0:M, 0:nt], func=AF.Exp, scale=-0.5), reads=[lb], writes=[rb])
        return rt, rb

    def stat_ps(self):
        i = self._stat_i
        self._stat_i = (i + 1) % 2
        return self.ps[3 + i], self.psb[3 + i]

    def main_ps(self):
        i = self._main_i
        self._main_i = (i + 1) % 3
        return self.ps[i], self.psb[i]

    def rope_ps(self):
        i = self._rope_i
        self._rope_i = (i + 1) % 2
        return self.ps[5 + i], self.psb[5 + i]

    def norm_store(self, raw_list, M, ones_ap, inv_d, gcol_list, nt, wk, outs, rope=None):
        P = self.P
        rt, rb = self.rstd_from([(sq, bs) for (_, sq, _, bs) in raw_list], M, ones_ap, inv_d, nt, wk)
        for (raw, _, braw, _), gcol, (dst, dbuf) in zip(raw_list, gcol_list, outs):
            if rope is None:
                P.emit("dve", lambda e, raw=raw, gcol=gcol, dst=dst: e.scalar_tensor_tensor(out=dst, in0=raw, scalar=gcol, in1=rt[0:M, 0:nt], op0=ALU.mult, op1=ALU.mult), reads=[braw, rb, self.cb], writes=[dbuf])
            else:
                perm, cos_ap, sin_ap, tbuf = rope
                nt_, nb_ = wk.next()
                P.emit("dve", lambda e, raw=raw, gcol=gcol, nt_=nt_: e.scalar_tensor_tensor(out=nt_[0:M, 0:nt], in0=raw, scalar=gcol, in1=rt[0:M, 0:nt], op0=ALU.mult, op1=ALU.mult), reads=[braw, rb, self.cb], writes=[nb_])
                rps, rpb = self.rope_ps()
                P.emit("pe", lambda e, nt_=nt_, rps=rps: e.matmul(rps[0:M, 0:nt], lhsT=perm, rhs=nt_[0:M, 0:nt], start=True, stop=True), reads=[nb_, self.cb], writes=[rpb])
                t1, b1 = wk.next()
                P.emit("dve", lambda e, nt_=nt_, t1=t1: e.tensor_tensor(out=t1[0:M, 0:nt], in0=nt_[0:M, 0:nt], in1=cos_ap, op=ALU.mult), reads=[nb_, tbuf], writes=[b1])
                t2, b2 = wk.next()
                P.emit("dve", lambda e, rps=rps, t2=t2: e.tensor_tensor(out=t2[0:M, 0:nt], in0=rps[0:M, 0:nt], in1=sin_ap, op=ALU.mult), reads=[rpb, tbuf], writes=[b2])
                P.emit("pool", lambda e, t1=t1, t2=t2, dst=dst: e.tensor_tensor(out=dst, in0=t1[0:M, 0:nt], in1=t2[0:M, 0:nt], op=ALU.add), reads=[b1, b2], writes=[dbuf])

    def evac_raw_sq(self, ps_ap, psbuf, M, nt, wk):
        P = self.P
        raw, braw = wk.next()
        sq, bsq = wk.next()
        P.emit("dve", lambda e: e.tensor_copy(out=raw[0:M, 0:nt], in_=ps_ap), reads=[psbuf], writes=[braw])
        P.emit("act", lambda e: e.activation(out=sq[0:M, 0:nt], in_=raw[0:M, 0:nt], func=AF.Square), reads=[braw], writes=[bsq])
        return (raw[0:M, 0:nt], sq[0:M, 0:nt], braw, bsq)

    def stage_a(self):
        nc, P, io, c = self.nc, self.P, self.io, self.c
        self._main_i = self._stat_i = self._rope_i = 0
        with ExitStack() as st:
            xblk = self.sb(st, "xblk", [128, 16, NB], F32)
            xb_ = Buf("xblk")
            hT = self.sb(st, "hT", [128, 16, NB], BF16)
            hb = Buf("hT")
            wring = Ring(nc, st, f"{self.ln}_wa", 3, [128, 16, 512], BF16)
            wk = Ring(nc, st, f"{self.ln}_wk", 14, [128, NB], F32)
            ob = Ring(nc, st, f"{self.ln}_ob", 6, [128, NB], BF16)
            rtab = self.sb(st, "rtab", [128, 4, NB], F32)
            rtb = Buf("rtab")
            cq = self.sb(st, "cq", [128, 4, NB], BF16)
            cqb = Buf("cq")
            ckv = self.sb(st, "ckv", [128, 2, NB], BF16)
            ckvb = Buf("ckv")
            rawm = self.sb(st, "rawm", [128, 4, NB], F32)
            sqm = self.sb(st, "sqm", [128, 4, NB], F32)
            rawmb = [Buf() for _ in range(4)]
            sqmb = [Buf() for _ in range(4)]
            vst = [self.sb(st, f"vst{i}", [128, NVH, 129], BF16) for i in range(4)]
            vsb = [Buf(f"vst{i}") for i in range(4)]
            for i in range(4):
                P.emit("pool", lambda e, i=i: e.memset(vst[i][:], 1.0), writes=[vsb[i]])

            blocks = []
            for i in range(4):
                blocks.append(dict(src=io["xo"][:, i * NB:(i + 1) * NB], nt=NB, j=0, full=True, rope=io["rope_own"][:, :, i * NB:(i + 1) * NB], q0=i * NB, k0=i * NB, na=True))
            for i in range(4):
                blocks.append(dict(src=io["xt"][:, i * NB:(i + 1) * NB], nt=NB, j=0, full=False, rope=io["rope_oth"][:, :, i * NB:(i + 1) * NB], q0=None, k0=HALF + i * NB, na=(i in (0, 3))))
            blocks.append(dict(src=io["cx"], nt=CTX, j=1, full=self.need_ctx, rope=None, q0=HALF, k0=SEQ, na=True))

            for blk in blocks:
              try:
                nt, j, full = blk["nt"], blk["j"], blk["full"]
                nts = nt // 128
                P.dma("sp", lambda e, blk=blk, nt=nt: e.dma_start(out=xblk[:, :, 0:nt], in_=blk["src"].rearrange("(k p) t -> p k t", p=128)), writes=[xb_])
                if blk["rope"] is not None:
                    P.dma("sp", lambda e, blk=blk, nt=nt: e.dma_start(out=rtab[:, :, 0:nt], in_=blk["rope"].rearrange("f p t -> p f t")), writes=[rtb])
                self.dbg("a_load")
                self.norm_block(xblk, xb_, nt, j, c["a1"], 0, hT, hb, wk)
                self.dbg("a_norm")

                def fm_group(c0, ncols):
                    wt, wb = self.load_w(wring, io["w_in"][:, c0:c0 + ncols], ncols)
                    res = []
                    for m0 in range(0, ncols, 128):
                        M = min(128, ncols - m0)
                        ps, pb = self.main_ps()
                        fns = [lambda e, k=k, m0=m0, M=M, ps=ps, wt=wt: e.matmul(ps[0:M, 0:nt], lhsT=wt[:, k, m0:m0 + M], rhs=hT[:, k, 0:nt], start=(k == 0), stop=(k == 15)) for k in range(16)]
                        P.emit("pe", fns, reads=[wb, hb], writes=[pb])
                        res.append((ps, pb, M))
                        yield (ps, pb, M)

                def store_q(dst_rows, M, col0, t, tb):
                    P.dma("sp", lambda e: e.dma_start(out=io["QT"][dst_rows:dst_rows + M, col0:col0 + nt], in_=t[0:M, 0:nt]), reads=[tb])

                def store_k(dst_rows, M, col0, t, tb):
                    P.dma("sp", lambda e: e.dma_start(out=io["KT"][dst_rows:dst_rows + M, col0:col0 + nt], in_=t[0:M, 0:nt]), reads=[tb])

                ropeG = None if blk["rope"] is None else (c["pg"][:], rtab[:, 0, 0:nt], rtab[:, 1, 0:nt], rtb)
                ropeD = None if blk["rope"] is None else (c["pd"][:], rtab[:, 2, 0:nt], rtab[:, 3, 0:nt], rtb)
                ropeM = None if blk["rope"] is None else (c["pd"][0:64, 0:64], rtab[0:64, 2, 0:nt], rtab[0:64, 3, 0:nt], rtb)

                def simple_heads(c0, nheads, ones_ap, inv_d, gcol, rope, store, row0, col0):
                    for hh, (ps, pb, M) in enumerate(fm_group(c0, nheads * 128)):
                        r = self.evac_raw_sq(ps[0:128, 0:nt], pb, 128, nt, wk)
                        ot, otb = ob.next()
                        self.norm_store([r], 128, ones_ap, inv_d, [gcol], nt, wk, [(ot[:, 0:nt], otb)], rope=rope)
                        store(row0 + hh * 128, 128, col0, ot, otb)

                gc = c["gc"]
                if full:
                    for kc, (ps, pb, M) in enumerate(fm_group(0, 512)):
                        P.emit("dve", lambda e, kc=kc, ps=ps: e.tensor_copy(out=rawm[:, kc, 0:nt], in_=ps[:, 0:nt]), reads=[pb], writes=[rawmb[kc]])
                        P.emit("act", lambda e, kc=kc, ps=ps: e.activation(out=sqm[:, kc, 0:nt], in_=rawm[:, kc, 0:nt], func=AF.Square), reads=[rawmb[kc]], writes=[sqmb[kc]])
                    self.dbg("a_q0")
                    self.norm_store([(rawm[:, kc, 0:nt], sqm[:, kc, 0:nt], rawmb[kc], sqmb[kc]) for kc in range(4)], 128, c["ones"][:], 1.0 / 512,
                                    [gc[:, GC_QA + kc:GC_QA + kc + 1] for kc in range(4)], nt, wk, [(cq[:, kc, 0:nt], cqb) for kc in range(4)])
                    self.dbg("a_q1")
                    for hh in range(4):
                        ps, pb = self.main_ps()
                        fns = [lambda e, kc=kc, hh=hh, ps=ps: e.matmul(ps[:, 0:nt], lhsT=c["wuq"][:, kc, hh * 192:hh * 192 + 128], rhs=cq[:, kc, 0:nt], start=(kc == 0), stop=(kc == 3)) for kc in range(4)]
                        P.emit("pe", fns, reads=[cqb, self.cb], writes=[pb])
                        r = self.evac_raw_sq(ps[0:128, 0:nt], pb, 128, nt, wk)
                        ot, otb = ob.next()
                        self.norm_store([r], 128, c["ones"][:], 1.0 / 128, [gc[:, GC_MQN:GC_MQN + 1]], nt, wk, [(ot[:, 0:nt], otb)])
                        store_q(Q_MLA_N + hh * 128, 128, blk["q0"], ot, otb)
                        self.dbg("a_q2")
                        ps, pb = self.main_ps()
                        fns = [lambda e, kc=kc, hh=hh, ps=ps: e.matmul(ps[0:64, 0:nt], lhsT=c["wuq"][:, kc, hh * 192 + 128:hh * 192 + 192], rhs=cq[:, kc, 0:nt], start=(kc == 0), stop=(kc == 3)) for kc in range(4)]
                        P.emit("pe", fns, reads=[cqb, self.cb], writes=[pb])
                        r = self.evac_raw_sq(ps[0:64, 0:nt], pb, 64, nt, wk)
                        ot, otb = ob.next()
                        self.norm_store([r], 64, c["ones"][0:64, 0:64], 1.0 / 64, [gc[0:64, GC_MQR:GC_MQR + 1]], nt, wk, [(ot[0:64, 0:nt], otb)], rope=ropeM)
                        store_q(Q_MLA_R + hh * 64, 64, blk["q0"], ot, otb)
                        self.dbg("a_q3")
                    self.dbg("a_mlaq")
                    simple_heads(832, 4, c["ones"][:], 1.0 / 128, gc[:, GC_GQ:GC_GQ + 1], ropeG, store_q, Q_GQA, blk["q0"])
                    self.dbg("a_gq")
                    simple_heads(1856, 4, c["bd64"][:], 1.0 / 64, gc[:, GC_DQ:GC_DQ + 1], ropeD, store_q, Q_DIFF, blk["q0"])
                    simple_heads(3392, 4, c["ones"][:], 1.0 / 128, gc[:, GC_NQ:GC_NQ + 1], None, store_q, Q_NA, blk["q0"])

                self.dbg("a_q")
                for kc, (ps, pb, M) in enumerate(fm_group(512, 320)):
                    if kc < 2:
                        P.emit("dve", lambda e, kc=kc, ps=ps: e.tensor_copy(out=rawm[:, kc, 0:nt], in_=ps[:, 0:nt]), reads=[pb], writes=[rawmb[kc]])
                        P.emit("act", lambda e, kc=kc, ps=ps: e.activation(out=sqm[:, kc, 0:nt], in_=rawm[:, kc, 0:nt], func=AF.Square), reads=[rawmb[kc]], writes=[sqmb[kc]])
                    else:
                        r = self.evac_raw_sq(ps[0:64, 0:nt], pb, 64, nt, wk)
                        ot, otb = ob.next()
                        self.norm_store([r], 64, c["ones"][0:64, 0:64], 1.0 / 64, [gc[0:64, GC_MKR:GC_MKR + 1]], nt, wk, [(ot[0:64, 0:nt], otb)], rope=ropeM)
                        store_k(K_MLA_R, 64, blk["k0"], ot, otb)
                self.norm_store([(rawm[:, kc, 0:nt], sqm[:, kc, 0:nt], rawmb[kc], sqmb[kc]) for kc in range(2)], 128, c["ones"][:], 1.0 / 256,
                                [gc[:, GC_KVA + kc:GC_KVA + kc + 1] for kc in range(2)], nt, wk, [(ckv[:, kc, 0:nt], ckvb) for kc in range(2)])
                for hh in range(4):
                    ps, pb = self.main_ps()
                    fns = [lambda e, kc=kc, hh=hh, ps=ps: e.matmul(ps[:, 0:nt], lhsT=c["wukv"][:, kc, hh * 256:hh * 256 + 128], rhs=ckv[:, kc, 0:nt], start=(kc == 0), stop=(kc == 1)) for kc in range(2)]
                    P.emit("pe", fns, reads=[ckvb, self.cb], writes=[pb])
                    r = self.evac_raw_sq(ps[0:128, 0:nt], pb, 128, nt, wk)
                    ot, otb = ob.next()
                    self.norm_store([r], 128, c["ones"][:], 1.0 / 128, [gc[:, GC_MKN:GC_MKN + 1]], nt, wk, [(ot[:, 0:nt], otb)])
                    store_k(K_MLA_N + hh * 128, 128, blk["k0"], ot, otb)
                wv = c["wukv"][:].rearrange("p k (h t d) -> p k h t d", h=4, t=2)
                for ts in range(nts):
                    ps, pb = self.main_ps()
                    fns = [lambda e, kc=kc, ts=ts, ps=ps: e.matmul(ps[:, 0:512].rearrange("p (h d) -> p h d", h=4), lhsT=ckv[:, kc, ts * 128:(ts + 1) * 128], rhs=wv[:, kc, :, 1, :], start=(kc == 0), stop=(kc == 1)) for kc in range(2)]
                    P.emit("pe", fns, reads=[ckvb, self.cb], writes=[pb])
                    P.emit("act", lambda e, ts=ts, ps=ps: e.activation(out=vst[ts][:, 0:4, 0:128], in_=ps[:, 0:512].rearrange("p (h d) -> p h d", h=4), func=AF.Copy), reads=[pb], writes=[vsb[ts]])

                self.dbg("a_mlakv")
                simple_heads(1344, 2, c["ones"][:], 1.0 / 128, gc[:, GC_GK:GC_GK + 1], ropeG, store_k, K_GQA, blk["k0"])
                simple_heads(2368, 4, c["bd64"][:], 1.0 / 64, gc[:, GC_DK:GC_DK + 1], ropeD, store_k, K_DIFF, blk["k0"])
                if blk["na"]:
                    simple_heads(3904, 4, c["ones"][:], 1.0 / 128, gc[:, GC_NK:GC_NK + 1], None, store_k, K_NA, blk["k0"])

                self.dbg("a_k")
                vgroups = [(1600, 2, 4), (2880, 4, 6)]
                if blk["na"]:
                    vgroups.append((4416, 4, 10))
                for (c0, nh, h0) in vgroups:
                    wt, wb = self.load_w(wring, io["w_in"][:, c0:c0 + nh * 128], nh * 128)
                    for ts in range(nts):
                        ps, pb = self.main_ps()
                        fns = [lambda e, k=k, ts=ts, ps=ps, wt=wt, nh=nh: e.matmul(ps[:, 0:nh * 128], lhsT=hT[:, k, ts * 128:(ts + 1) * 128], rhs=wt[:, k, 0:nh * 128], start=(k == 0), stop=(k == 15)) for k in range(16)]
                        P.emit("pe", fns, reads=[wb, hb], writes=[pb])
                        P.emit("act", lambda e, ts=ts, ps=ps, nh=nh, h0=h0: e.activation(out=vst[ts][:, h0:h0 + nh, 0:128], in_=ps[:, 0:nh * 128].rearrange("p (h d) -> p h d", h=nh), func=AF.Copy), reads=[pb], writes=[vsb[ts]])
                self.dbg("a_v1")
                P.barrier()
                for ts in range(nts):
                    r0 = blk["k0"] + ts * 128
                    P.dma("sp", lambda e, ts=ts, r0=r0: e.dma_start(out=io["V"][r0:r0 + 128, :], in_=vst[ts][:].rearrange("p h d -> p (h d)")), reads=[vsb[ts]])
                self.dbg("a_blk")
                self.dbg("a_blkB")
              except _Stop:
                break
            P.barrier()
            P.flush()

    def norm_block(self, xblk, xb_, nt, j, a_tile, sh_which, out_t, out_b, wk):
        P, c = self.P, self.c
        sqs = []
        for k in range(16):
            sq, bsq = wk.next()
            if k % 2 == 0:
                P.emit("act", lambda e: e.activation(out=sq[:, 0:nt], in_=xblk[:, k, 0:nt], func=AF.Square), reads=[xb_], writes=[bsq])
            else:
                P.emit("pool", lambda e: e.tensor_tensor(out=sq[:, 0:nt], in0=xblk[:, k, 0:nt], in1=xblk[:, k, 0:nt], op=ALU.mult), reads=[xb_], writes=[bsq])
            sqs.append((sq[:, 0:nt], bsq))
            if len(sqs) == 4:
                self._stat_acc(sqs, k - 3, nt)
                sqs = []
        self.dbg("n_sq")
        rt, rb = self._stat_fin(nt, None)
        self.dbg("n_fin")
        for k in range(16):
            t1, b1 = wk.next()
            P.emit("dve", lambda e: e.scalar_tensor_tensor(out=t1[:, 0:nt], in0=xblk[:, k, 0:nt], scalar=a_tile[:, k, j:j + 1], in1=rt[:, 0:nt], op0=ALU.mult, op1=ALU.mult), reads=[xb_, rb, self.cb], writes=[b1])
            self.dbg("n_dve1")
            if self.stop == "n_act0":
                P.emit("act", lambda e: e.activation(out=out_t[:, k, 0:nt], in_=t1[:, 0:nt], func=AF.Identity), reads=[b1, self.cb], writes=[out_b])
                self.dbg("n_act0")
            P.emit("act", lambda e: e.activation(out=out_t[:, k, 0:nt], in_=t1[:, 0:nt], func=AF.Identity, bias=self.mod(sh_which, k, j), scale=1.0), reads=[b1, self.cb], writes=[out_b])
            self.dbg("n_act1")
            if self.stop and self.stop.startswith("n_k") and int(self.stop[3:]) == k:
                raise _Stop()

    def attention(self):
        nc, P, io, c = self.nc, self.P, self.io, self.c
        with ExitStack() as st:
            kt = self.sb(st, "kt", [128, 4, NK_TOK], BF16)
            ktb = Buf("kt")
            kr = self.sb(st, "kr", [64, NK_TOK], BF16)
            krb = Buf("kr")
            vt = self.sb(st, "vt", [128, NKT, 4 * 129], BF16)
            vtb = Buf("vt")
            qring = Ring(nc, st, f"{self.ln}_q", 2, [128, 4, NB], BF16)
            qrring = Ring(nc, st, f"{self.ln}_qr", 2, [64, 4, NB], BF16)
            pring = Ring(nc, st, f"{self.ln}_p", 4, [128, NB], BF16)
            sring = Ring(nc, st, f"{self.ln}_s", 3, [128, 128], F32)
            oring = Ring(nc, st, f"{self.ln}_o", 4, [128, 128], BF16)
            o1 = self.sb(st, "dfo1", [128, 4, 128], F32)
            o1b = Buf("o1")
            wkf = Ring(nc, st, f"{self.ln}_af", 4, [128, 128], F32)
            small = Ring(nc, st, f"{self.ln}_sm", 8, [128, 4], F32)
            otst = Ring(nc, st, f"{self.ln}_ot", 3, [128, NB], BF16)
            nab = self.sb(st, "nab", [128, 4, 7, 128], F32)
            nabb = Buf("nab")
            self._s_i = 0

            def s_ps():
                i = self._s_i
                self._s_i = (i + 1) % 3
                return self.ps[i], self.psb[i]

            def load_k(row0, nheads):
                for hh in range(nheads):
                    P.dma("sp", lambda e, hh=hh: e.dma_start(out=kt[:, hh, :], in_=io["KT"][row0 + hh * 128:row0 + (hh + 1) * 128, :]), writes=[ktb])

            def load_v(h0, nh):
                P.dma("sp", lambda e: e.dma_start(out=vt[:, :, 0:nh * 129], in_=io["V"][:, h0 * 129:(h0 + nh) * 129].rearrange("(t p) f -> p t f", p=128)), writes=[vtb])

            def load_q(row0, nheads, col0, nq, ring, M=128):
                qt, qb = ring.next()
                P.dma("sp", lambda e: e.dma_start(out=qt[0:M, 0:nheads, 0:nq], in_=io["QT"][row0:row0 + nheads * M, col0:col0 + nq].rearrange("(h p) t -> p h t", p=M)), writes=[qb])
                return qt, qb

            def attend(pieces, vh, ktiles, nq, scale, bias=None):
                nqs = nq // 128
                rbufs = [b for p_ in pieces for b in p_[2]]
                def s_mm(ks):
                    sp_, spb = s_ps()
                    fns = [lambda e, i=i, pc=pc: e.matmul(sp_[:, 0:nq], lhsT=pc[0](ks), rhs=pc[1], start=(i == 0), stop=(i == len(pieces) - 1)) for i, pc in enumerate(pieces)]
                    P.emit("pe", fns, reads=rbufs, writes=[spb])
                    return sp_, spb

                nxt = s_mm(ktiles[0])
                for ti, ks in enumerate(ktiles):
                    sp_, spb = nxt
                    if ti + 1 < len(ktiles):
                        nxt = s_mm(ktiles[ti + 1])
                    pt, ptb = pring.next()
                    bap = bias(ti) if bias is not None else None
                    if bap is not None:
                        tmp, tmpb = sring.next()
                        P.emit("dve", lambda e: e.scalar_tensor_tensor(out=tmp[:, 0:nq], in0=sp_[:, 0:nq], scalar=float(scale), in1=bap, op0=ALU.mult, op1=ALU.add), reads=[spb, nabb], writes=[tmpb])
                        P.emit("act", lambda e: e.activation(out=pt[:, 0:nq], in_=tmp[:, 0:nq], func=AF.Exp), reads=[tmpb], writes=[ptb])
                    else:
                        P.emit("act", lambda e: e.activation(out=pt[:, 0:nq], in_=sp_[:, 0:nq], func=AF.Exp, scale=float(scale)), reads=[spb], writes=[ptb])
                    fns = [lambda e, qs=qs: e.matmul(self.ps[3 + qs][:, 0:129], lhsT=pt[:, qs * 128:(qs + 1) * 128], rhs=vt[:, ks, vh * 129:(vh + 1) * 129], start=(ti == 0), stop=(ti == len(ktiles) - 1)) for qs in range(nqs)]
                    P.emit("pe", fns, reads=[ptb, vtb], writes=[self.psb[3 + qs] for qs in range(nqs)])
                return [(self.ps[3 + qs], self.psb[3 + qs]) for qs in range(nqs)]

            def normalize(acc, accb, dst_ap, dst_b):
                sm, smb = small.next()
                P.emit("dve", lambda e: e.reciprocal(out=sm[:, 0:1], in_=acc[:, 128:129]), reads=[accb], writes=[smb])
                P.emit("dve", lambda e: e.tensor_scalar(out=dst_ap, in0=acc[:, 0:128], scalar1=sm[:, 0:1], scalar2=None, op0=ALU.mult), reads=[accb, smb], writes=[dst_b])

            def emit_oT(o_bf, o_b, ott, ottb, qs):
                tp = self.ps7
                P.emit("pe", lambda e: e.transpose(out=tp[:, 0:128], in_=o_bf[:], identity=c["ident"][:]), reads=[o_b, self.cb], writes=[self.ps7b])
                P.emit("act", lambda e: e.activation(out=ott[:, qs * 128:(qs + 1) * 128], in_=tp[:, 0:128], func=AF.Copy), reads=[self.ps7b], writes=[ottb])

            def store_oT(ott, ottb, row0, col0, nq):
                P.dma("sp", lambda e: e.dma_start(out=io["OT"][row0:row0 + 128, col0:col0 + nq], in_=ott[:, 0:nq]), reads=[ottb])

            all_k = list(range(NKT))
            qblocks = [(i * NB, NB, all_k) for i in range(4)]
            if self.need_ctx:
                qblocks.append((HALF, CTX, [32, 33]))

            def std_finish(accs, row0, col0, nq):
                ott, ottb = otst.next()
                for qs, (acc, accb) in enumerate(accs):
                    ot_, ob_ = oring.next()
                    normalize(acc, accb, ot_[:], ob_)
                    emit_oT(ot_, ob_, ott, ottb, qs)
                store_oT(ott, ottb, row0, col0, nq)

            load_k(K_MLA_N, 4)
            P.dma("sp", lambda e: e.dma_start(out=kr[:], in_=io["KT"][K_MLA_R:K_MLA_R + 64, :]), writes=[krb])
            load_v(0, 4)
            sc = 192.0 ** -0.5
            for (col0, nq, ktl) in qblocks:
                qn, qnb = load_q(Q_MLA_N, 4, col0, nq, qring)
                qr, qrb = load_q(Q_MLA_R, 4, col0, nq, qrring, M=64)
                for hh in range(4):
                    pieces = [(lambda ks, hh=hh: kt[:, hh, ks * 128:(ks + 1) * 128], qn[:, hh, 0:nq], [ktb, qnb]),
                              (lambda ks: kr[:, ks * 128:(ks + 1) * 128], qr[:, hh, 0:nq], [krb, qrb])]
                    accs = attend(pieces, hh, ktl, nq, sc)
                    std_finish(accs, hh * 128, col0, nq)
            load_k(K_GQA, 2)
            load_v(4, 2)
            sc = 128.0 ** -0.5
            for (col0, nq, ktl) in qblocks:
                q, qb = load_q(Q_GQA, 4, col0, nq, qring)
                for hh in range(4):
                    pieces = [(lambda ks, hh=hh: kt[:, hh // 2, ks * 128:(ks + 1) * 128], q[:, hh, 0:nq], [ktb, qb])]
                    accs = attend(pieces, hh // 2, ktl, nq, sc)
                    std_finish(accs, 512 + hh * 128, col0, nq)
            load_k(K_DIFF, 4)
            load_v(6, 4)
            sc = 64.0 ** -0.5
            for (col0, nq, ktl) in qblocks:
                q, qb = load_q(Q_DIFF, 4, col0, nq, qring)
                for hh in range(4):
                    pieces = [(lambda ks, hh=hh: kt[0:64, hh, ks * 128:(ks + 1) * 128], q[0:64, hh, 0:nq], [ktb, qb])]
                    accs = attend(pieces, hh, ktl, nq, sc)
                    for qs, (acc, accb) in enumerate(accs):
                        normalize(acc, accb, o1[:, qs, :], o1b)
                    pieces = [(lambda ks, hh=hh: kt[64:128, hh, ks * 128:(ks + 1) * 128], q[64:128, hh, 0:nq], [ktb, qb])]
                    accs = attend(pieces, hh, ktl, nq, sc)
                    ott, ottb = otst.next()
                    for qs, (acc, accb) in enumerate(accs):
                        o2, o2b = wkf.next()
                        normalize(acc, accb, o2[:], o2b)
                        od, odb = wkf.next()
                        P.emit("dve", lambda e, o2=o2, od=od, qs=qs: e.scalar_tensor_tensor(out=od[:], in0=o2[:], scalar=c["lam"][:, 3:4], in1=o1[:, qs, :], op0=ALU.mult, op1=ALU.add), reads=[o2b, o1b, self.cb], writes=[odb])
                        sm, smb = small.next()
                        sq, sqb = wkf.next()
                        P.emit("act", lambda e, od=od, sq=sq, sm=sm: e.activation(out=sq[:], in_=od[:], func=AF.Square, accum_out=sm[:, 0:1]), reads=[odb], writes=[sqb, smb])
                        P.emit("act", lambda e, sm=sm: e.activation(out=sm[:, 1:2], in_=sm[:, 0:1], func=AF.Ln, bias=c["eps"][:], scale=1.0 / 128), reads=[smb, self.cb], writes=[smb])
                        P.emit("act", lambda e, sm=sm: e.activation(out=sm[:, 2:3], in_=sm[:, 1:2], func=AF.Exp, scale=-0.5), reads=[smb], writes=[smb])
                        P.emit("dve", lambda e, od=od, sm=sm: e.tensor_scalar(out=od[:], in0=od[:], scalar1=sm[:, 2:3], scalar2=float(1.0 - self.lam_init), op0=ALU.mult, op1=ALU.mult), reads=[odb, smb], writes=[odb])
                        ot_, ob_ = oring.next()
                        P.emit("dve", lambda e, od=od, ot_=ot_: e.tensor_tensor(out=ot_[:], in0=od[:], in1=c["grow"][:], op=ALU.mult), reads=[odb, self.cb], writes=[ob_])
                        emit_oT(ot_, ob_, ott, ottb, qs)
                    store_oT(ott, ottb, 1024 + hh * 128, col0, nq)
            load_k(K_NA, 4)
            load_v(10, 4)
            sc = 128.0 ** -0.5
            cur_pat = None
            for jt in range(16):
                pat = {0: 0, 1: 1, 14: 3, 15: 4}.get(jt, 2)
                if pat != cur_pat:
                    P.dma("sp", lambda e, pat=pat: e.dma_start(out=nab[:], in_=io["nab"][pat]), writes=[nabb])
                    cur_pat = pat
                q, qb = load_q(Q_NA, 4, jt * 128, 128, qring)
                ktl = [(jt + r) % 32 for r in range(-3, 4)] + [32, 33]
                ott, ottb = otst.next()
                for hh in range(4):
                    pieces = [(lambda ks, hh=hh: kt[:, hh, ks * 128:(ks + 1) * 128], q[:, hh, 0:128], [ktb, qb])]
                    accs = attend(pieces, hh, ktl, 128, sc, bias=lambda ti, hh=hh: (nab[:, hh, ti, :] if ti < 7 else None))
                    ot_, ob_ = oring.next()
                    normalize(accs[0][0], accs[0][1], ot_[:], ob_)
                    emit_oT(ot_, ob_, ott, ottb, hh)
                for hh in range(4):
                    P.dma("sp", lambda e, hh=hh, ott=ott, jt=jt: e.dma_start(out=io["OT"][1536 + hh * 128:1536 + (hh + 1) * 128, jt * 128:(jt + 1) * 128], in_=ott[:, hh * 128:(hh + 1) * 128]), reads=[ottb])
            if self.need_ctx:
                q, qb = load_q(Q_NA, 4, HALF, CTX, qring)
                for hh in range(4):
                    pieces = [(lambda ks, hh=hh: kt[:, hh, ks * 128:(ks + 1) * 128], q[:, hh, 0:CTX], [ktb, qb])]
                    accs = attend(pieces, hh, [32, 33], CTX, sc)
                    std_finish(accs, 1536 + hh * 128, HALF, CTX)
            P.barrier()
            P.flush()

    def stage_c(self):
        nc, P, io, c = self.nc, self.P, self.io, self.c
        self._main_i = self._stat_i = 0
        with ExitStack() as st:
            xblk = self.sb(st, "xc", [128, 16, NB], F32)
            xb_ = Buf("xc")
            oT = self.sb(st, "oTc", [128, 16, NB], BF16)
            oTb = Buf("oTc")
            h2 = self.sb(st, "h2", [128, 16, NB], BF16)
            h2b = Buf("h2")
            aT = self.sb(st, "aT", [128, 64, NB], BF16)
            aTb = [Buf(f"aT{i}") for i in range(16)]
            wring = Ring(nc, st, f"{self.ln}_wc", 2, [128, 16, 512], BF16)
            wk = Ring(nc, st, f"{self.ln}_wkc", 8, [128, NB], F32)
            blocks = [dict(x=io["xo"][:, i * NB:(i + 1) * NB], o=io["OT"][:, i * NB:(i + 1) * NB], y=io["yo"][:, i * NB:(i + 1) * NB], nt=NB, j=0) for i in range(4)]
            if self.need_ctx:
                blocks.append(dict(x=io["cx"], o=io["OT"][:, HALF:HALF + CTX], y=io["yc"], nt=CTX, j=1))
            for blk in blocks:
                nt, j = blk["nt"], blk["j"]
                P.dma("sp", lambda e, blk=blk, nt=nt: e.dma_start(out=xblk[:, :, 0:nt], in_=blk["x"].rearrange("(k p) t -> p k t", p=128)), writes=[xb_])
                P.dma("sp", lambda e, blk=blk, nt=nt: e.dma_start(out=oT[:, :, 0:nt], in_=blk["o"].rearrange("(k p) t -> p k t", p=128)), writes=[oTb])
                for g in range(4):
                    wt, wb = self.load_w(wring, io["w_out"][:, g * 512:(g + 1) * 512], 512)
                    for m in range(4):
                        n = g * 4 + m
                        ps, pb = self.main_ps()
                        fns = [lambda e, k=k, m=m, ps=ps, wt=wt: e.matmul(ps[:, 0:nt], lhsT=wt[:, k, m * 128:(m + 1) * 128], rhs=oT[:, k, 0:nt], start=(k == 0), stop=(k == 15)) for k in range(16)]
                        P.emit("pe", fns, reads=[wb, oTb], writes=[pb])
                        tg, tgb = wk.next()
                        P.emit("act", lambda e: e.activation(out=tg[:, 0:nt], in_=ps[:, 0:nt], func=AF.Copy, scale=self.mod(2, n, j)), reads=[pb, self.cb], writes=[tgb])
                        P.emit("dve", lambda e: e.tensor_tensor(out=xblk[:, n, 0:nt], in0=xblk[:, n, 0:nt], in1=tg[:, 0:nt], op=ALU.add), reads=[tgb, xb_], writes=[xb_])
                self.norm_block(xblk, xb_, nt, j, c["a2"], 3, h2, h2b, wk)
                for g in range(16):
                    wt, wb = self.load_w(wring, io["w_up"][:, g * 512:(g + 1) * 512], 512)
                    for m in range(4):
                        f = g * 4 + m
                        ps, pb = self.main_ps()
                        fns = [lambda e, k=k, m=m, ps=ps, wt=wt: e.matmul(ps[:, 0:nt], lhsT=wt[:, k, m * 128:(m + 1) * 128], rhs=h2[:, k, 0:nt], start=(k == 0), stop=(k == 15)) for k in range(16)]
                        P.emit("pe", fns, reads=[wb, h2b], writes=[pb])
                        r, rb_ = wk.next()
                        P.emit("act", lambda e, ps=ps, r=r: e.activation(out=r[:, 0:nt], in_=ps[:, 0:nt], func=AF.Relu), reads=[pb], writes=[rb_])
                        eng = "dve" if m % 2 == 0 else "pool"
                        P.emit(eng, lambda e, r=r, f=f: e.tensor_tensor(out=aT[:, f, 0:nt], in0=r[:, 0:nt], in1=r[:, 0:nt], op=ALU.mult), reads=[rb_], writes=[aTb[f // 4]])
                for g in range(4):
                    for kq in range(4):
                        wt, wb = self.load_w(wring, io["w_down"][kq * 2048:(kq + 1) * 2048, g * 512:(g + 1) * 512], 512)
                        for m in range(4):
                            fns = [lambda e, k=k, m=m, wt=wt, kq=kq: e.matmul(self.ps[m][:, 0:nt], lhsT=wt[:, k, m * 128:(m + 1) * 128], rhs=aT[:, kq * 16 + k, 0:nt], start=(kq == 0 and k == 0), stop=(kq == 3 and k == 15)) for k in range(16)]
                            P.emit("pe", fns, reads=[wb] + aTb[kq * 4:(kq + 1) * 4], writes=[self.psb[m]])
                    for m in range(4):
                        n = g * 4 + m
                        tg, tgb = wk.next()
                        P.emit("act", lambda e: e.activation(out=tg[:, 0:nt], in_=self.ps[m][:, 0:nt], func=AF.Copy, scale=self.mod(5, n, j)), reads=[self.psb[m], self.cb], writes=[tgb])
                        P.emit("dve", lambda e: e.tensor_tensor(out=xblk[:, n, 0:nt], in0=xblk[:, n, 0:nt], in1=tg[:, 0:nt], op=ALU.add), reads=[tgb, xb_], writes=[xb_])
                self._main_i = 0
                self.out_tickets.append(P.dma("sp", lambda e, blk=blk, nt=nt: e.dma_start(out=blk["y"].rearrange("(k p) t -> p k t", p=128), in_=xblk[:, :, 0:nt]), reads=[xb_]))
            P.barrier()
            P.flush()

    def _stat_acc(self, sqs, k0, nt):
        P = self.P
        sps, spb = self.ps[4], self.psb[4]
        fns = [lambda e, i=i, ap=ap: e.matmul(sps[:, 0:nt], lhsT=self.c["ones"][:], rhs=ap, start=(k0 + i == 0), stop=(k0 + i == 15)) for i, (ap, b) in enumerate(sqs)]
        P.emit("pe", fns, reads=[b for _, b in sqs] + [self.cb], writes=[spb])

    def _stat_fin(self, nt, wk):
        P, c = self.P, self.c
        sps, spb = self.ps[4], self.psb[4]
        lt, lb = self.nb_lt, self.nb_ltb
        P.emit("act", lambda e: e.activation(out=lt[:, 0:nt], in_=sps[:, 0:nt], func=AF.Ln, bias=c["eps"][:], scale=1.0 / D), reads=[spb, self.cb], writes=[lb])
        rt, rb = self.nb_rt, self.nb_rtb
        P.emit("act", lambda e: e.activation(out=rt[:, 0:nt], in_=lt[:, 0:nt], func=AF.Exp, scale=-0.5), reads=[lb], writes=[rb])
        return rt, rb


def build_layer(need_ctx, lam_init, debug=False, stop_after=None):
    nc = bass.Bass("TRN2", target_bir_lowering=False)

    def din(name, shape, dt=F32):
        return nc.dram_tensor(name, list(shape), dt, kind="ExternalInput").ap()

    io = {}
    io["xo"] = din("xo", [D, HALF])
    io["xt"] = din("xt", [D, HALF])
    io["cx"] = din("cx", [D, CTX])
    io["cc"] = din("cc", [128, 16, 2])
    io["w_mod"] = din("w_mod", [D, 6 * D])
    io["bmodT"] = din("bmodT", [128, 96])
    io["w_in"] = din("w_in", [D, IN_COLS])
    io["w_uq"] = din("w_uq", [512, 768])
    io["w_ukv"] = din("w_ukv", [256, 1024])
    io["w_out"] = din("w_out", [D, D])
    io["w_up"] = din("w_up", [D, D_FF])
    io["w_down"] = din("w_down", [D_FF, D])
    io["gcols"] = din("gcols", [128, NGC])
    io["grow"] = din("grow", [128, 128])
    io["lamv"] = din("lamv", [128, 4, 64])
    io["cmat"] = din("cmat", [4, 128, 128])
    io["ident"] = din("ident", [128, 128], BF16)
    io["rope_own"] = din("rope_own", [4, 128, HALF])
    io["rope_oth"] = din("rope_oth", [4, 128, HALF])
    io["nab"] = din("nab", [5, 128, 4, 7, 128])
    io["yo"] = nc.dram_tensor("yo", [D, HALF], F32, kind="ExternalOutput").ap()
    io["yc"] = nc.dram_tensor("yc", [D, CTX], F32, kind="ExternalOutput").ap()
    skind = "ExternalOutput" if debug else "Internal"
    io["QT"] = nc.dram_tensor("QT", [NQ_ROWS, NQ_TOK], BF16, kind=skind).ap()
    io["KT"] = nc.dram_tensor("KT", [NK_ROWS, NK_TOK], BF16, kind=skind).ap()
    io["V"] = nc.dram_tensor("V", [NK_TOK, NVH * 129], BF16, kind=skind).ap()
    io["OT"] = nc.dram_tensor("OT", [D, NQ_TOK], BF16, kind=skind).ap()

    with ExitStack() as stack:
        P = Prog(nc, stack)
        L = LayerEmitter(nc, P, stack, io, need_ctx, lam_init, "L")
        L.ps = [stack.enter_context(nc.psum_tensor(f"ps{i}", [128, 512], F32)) for i in range(7)]
        L.psb = [Buf(f"ps{i}") for i in range(7)]
        L.ps7 = stack.enter_context(nc.psum_tensor("ps7", [128, 1024], BF16))
        L.ps7b = Buf("ps7")
        L.out_tickets = []
        L.stop = stop_after
        with ExitStack() as cst:
            L.setup(cst)
            sa = stop_after or ""
            if sa != "setup":
                L.stage_a()
            if sa not in ("setup", "a") and not sa.startswith("a_") and not sa.startswith("n_"):
                L.attention()
            if sa not in ("setup", "a", "attn") and not sa.startswith("a_") and not sa.startswith("n_"):
                L.stage_c()
            if not need_ctx:
                with ExitStack() as s3:
                    t = L.sb(s3, "cpass", [128, 16, CTX], F32)
                    tb = Buf()
                    P.dma("sp", lambda e: e.dma_start(out=t[:], in_=io["cx"].rearrange("(k p) t -> p k t", p=128)), writes=[tb])
                    L.out_tickets.append(P.dma("sp", lambda e: e.dma_start(out=io["yc"].rearrange("(k p) t -> p k t", p=128), in_=t[:]), reads=[tb]))
                    P.barrier()
                    P.flush()
        print(f"[build] instr={P.n_instr} waits={P.n_wait}")
    return nc


def _rope_tables():
    t = np.arange(SEQ)
    row = (t // GRID_W).astype(np.float64)
    col = (t % GRID_W).astype(np.float64)
    out = np.zeros((4, 128, SEQ), np.float64)
    inv = 10000.0 ** (-np.arange(0, 64, 2, dtype=np.float64) / 64)
    for p in range(128):
        d = p % 64
        pos = row if p < 64 else col
        ang = pos * inv[d % 32]
        out[0, p] = np.cos(ang)
        out[1, p] = np.sin(ang) * (-1.0 if d < 32 else 1.0)
    inv = 10000.0 ** (-np.arange(0, 32, 2, dtype=np.float64) / 32)
    for p in range(128):
        d = p % 64
        e = d % 32
        pos = row if d < 32 else col
        ang = pos * inv[e % 16]
        out[2, p] = np.cos(ang)
        out[3, p] = np.sin(ang) * (-1.0 if e < 16 else 1.0)
    return out.astype(np.float32)


def _const_mats():
    ones = np.ones((128, 128), np.float32)
    bd = np.zeros((128, 128), np.float32)
    bd[:64, :64] = 1
    bd[64:, 64:] = 1
    pg = np.zeros((128, 128), np.float32)
    for p in range(128):
        q = p + 32 if (p % 64) < 32 else p - 32
        pg[q, p] = 1.0
    pd = np.zeros((128, 128), np.float32)
    for p in range(128):
        q = p + 16 if (p % 32) < 16 else p - 16
        pd[q, p] = 1.0
    return np.stack([ones, bd, pg, pd])


def _na_tables(rpb, half):
    out = np.full((5, 128, 4, 7, 128), NEG, np.float32)
    for pi, jt in enumerate([0, 1, 7, 14, 15]):
        i = half * 16 + jt
        ql = np.arange(128)
        q_row = 2 * i + ql // 64
        q_col = ql % 64
        row_start = np.clip(q_row - 4, 0, 64 - 8)
        col_start = np.clip(q_col - 8, 0, 64 - 16)
        for r in range(7):
            ta = i - 3 + r
            if ta < 0 or ta > 31:
                continue
            kl = np.arange(128)
            k_row = 2 * ta + kl // 64
            k_col = kl % 64
            m = ((k_col[:, None] >= col_start[None, :]) & (k_col[:, None] < col_start[None, :] + 16)
                 & (k_row[:, None] >= row_start[None, :]) & (k_row[:, None] < row_start[None, :] + 8))
            ri = np.clip(k_row[:, None] - q_row[None, :] + 7, 0, 14)
            ci = np.clip(k_col[:, None] - q_col[None, :] + 15, 0, 30)
            for h in range(4):
                g = rpb[h][ri, ci]
                out[pi, :, h, r, :] = np.where(m, g, np.float32(NEG))
    return out


def _colvec(v, n=128):
    v = np.asarray(v, np.float32)
    if v.shape[0] == 64:
        return np.concatenate([v, v])[:, None]
    return np.ascontiguousarray(v.reshape(-1, 128).T)


def _layer_inputs(l, inp, b, half, xT_own, xT_oth, cxT, consts):
    f = np.float32
    gc = np.zeros((128, NGC), f)
    gc[:, GC_NMIX:GC_NMIX + 16] = _colvec(inp["g_norm_mix"][l])
    gc[:, GC_NMLP:GC_NMLP + 16] = _colvec(inp["g_norm_mlp"][l])
    gc[:, GC_QA:GC_QA + 4] = _colvec(inp["mla_g_qa"][l])
    gc[:, GC_KVA:GC_KVA + 2] = _colvec(inp["mla_g_kva"][l])
    gc[:, GC_MQN] = inp["mla_g_q"][l][:128]
    gc[:64, GC_MQR] = inp["mla_g_q"][l][128:]
    gc[:, GC_MKN] = inp["mla_g_k"][l][:128]
    gc[:64, GC_MKR] = inp["mla_g_k"][l][128:]
    gc[:, GC_GQ] = inp["gqa_g_q"][l]
    gc[:, GC_GK] = inp["gqa_g_k"][l]
    gc[:, GC_DQ] = np.concatenate([inp["diff_g_q"][l]] * 2)
    gc[:, GC_DK] = np.concatenate([inp["diff_g_k"][l]] * 2)
    gc[:, GC_NQ] = inp["na_g_q"][l]
    gc[:, GC_NK] = inp["na_g_k"][l]
    lamv = np.stack([np.broadcast_to(inp[k][l], (128, 64)) for k in ("diff_lq1", "diff_lk1", "diff_lq2", "diff_lk2")], axis=1)
    cc = np.stack([_colvec(inp["c"][b]), _colvec(inp["c_ctx"])], axis=2)
    rope = consts["rope"]
    own = slice(half * HALF, (half + 1) * HALF)
    oth = slice((1 - half) * HALF, (2 - half) * HALF)
    return {
        "xo": xT_own, "xt": xT_oth, "cx": cxT,
        "cc": np.ascontiguousarray(cc, f),
        "w_mod": inp["w_mod"][l], "bmodT": _colvec(inp["b_mod"][l]),
        "w_in": inp["w_in"][l], "w_uq": inp["mla_w_uq"][l], "w_ukv": inp["mla_w_ukv"][l],
        "w_out": inp["w_out"][l], "w_up": inp["w_up"][l], "w_down": inp["w_down"][l],
        "gcols": gc, "grow": np.ascontiguousarray(np.broadcast_to(inp["diff_g_out"][l], (128, 128)), f),
        "lamv": np.ascontiguousarray(lamv, f),
        "cmat": consts["cmat"], "ident": consts["ident"],
        "rope_own": np.ascontiguousarray(rope[:, :, own]), "rope_oth": np.ascontiguousarray(rope[:, :, oth]),
        "nab": consts["nab"][l][half],
    }


_PROGS = {}


def kernel(**inputs):
    inp = {k: np.asarray(v) for k, v in inputs.items()}
    depth = inp["w_in"].shape[0]
    consts = {
        "rope": _rope_tables(),
        "cmat": _const_mats(),
        "ident": np.eye(128, dtype=np.float32).astype(ml_dtypes.bfloat16),
        "nab": [[_na_tables(inp["na_rpb"][l], h) for h in range(2)] for l in range(depth)],
    }
    xT = [np.ascontiguousarray(inp["x"][b].T) for b in range(BATCH)]
    cT = [np.ascontiguousarray(inp["ctx"][b].T) for b in range(BATCH)]
    for l in range(depth):
        need_ctx = l < depth - 1
        lam_init = 0.8 - 0.6 * math.exp(-0.3 * l)
        key = (need_ctx, lam_init)
        if key not in _PROGS:
            _PROGS[key] = build_layer(need_ctx, lam_init)
        nc = _PROGS[key]
        in_maps = []
        for core in range(8):
            b, half = core // 2, core % 2
            own = slice(half * HALF, (half + 1) * HALF)
            oth = slice((1 - half) * HALF, (2 - half) * HALF)
            in_maps.append(_layer_inputs(l, inp, b, half, np.ascontiguousarray(xT[b][:, own]), np.ascontiguousarray(xT[b][:, oth]), cT[b], consts))
        res = run_bass_kernel_spmd(nc, in_maps, core_ids=list(range(8)))
        for b in range(BATCH):
            xT[b] = np.concatenate([res.results[2 * b]["yo"], res.results[2 * b + 1]["yo"]], axis=1)
            cT[b] = res.results[2 * b]["yc"]
    out = np.stack([xT[b].T for b in range(BATCH)], axis=0)
    return np.ascontiguousarray(out, dtype=np.float32)
```
